# Optimizing a Trainium2 kernel written in Bass

```python
import math
import jax, jax.numpy as jnp
from jax import lax
import numpy as np

D_MODEL = 4096
BATCH = 4
SEQ = 4096
DEPTH = 4

GRID_W = 64
CTX_LEN = 256
HEAD_DIM = 128
N_HEADS_TOTAL = D_MODEL // HEAD_DIM
A_Q_HEADS = N_HEADS_TOTAL // 2
A_KV_HEADS = A_Q_HEADS // 4
A_GROUP = A_Q_HEADS // A_KV_HEADS
B_HEADS = N_HEADS_TOTAL // 4
NA_ROWS = 8
NA_COLS = 16
C_HEADS = N_HEADS_TOTAL // 4
C_QK_DIM = HEAD_DIM // 2
C_V_DIM = HEAD_DIM
A_Q_W = A_Q_HEADS * HEAD_DIM
A_KV_W = A_KV_HEADS * HEAD_DIM
B_W = B_HEADS * HEAD_DIM
C_QK_W = C_HEADS * 2 * C_QK_DIM
C_V_W = C_HEADS * C_V_DIM
GATE_RANK = 512
IN_W = A_Q_W + 2 * A_KV_W + 3 * B_W + 2 * C_QK_W + C_V_W + GATE_RANK
MIX_W = A_Q_W + B_W + C_V_W
N_BRANCH = 3
ADA_RANK = 256
D_FF = 256 * ((8 * D_MODEL + 3 * 256 - 1) // (3 * 256))
Q_BLOCK = 128
ROPE_THETA = 10000.0
EPS = 1e-6

kernel_name = "hybrid_gqa_natten_diffattn_dit_trunk"


def _rms_norm(x, g):
    xf = x.astype(jnp.float32)
    y = xf * lax.rsqrt(jnp.mean(xf * xf, axis=-1, keepdims=True) + EPS)
    return (y * g.astype(jnp.float32)).astype(x.dtype)


def _adaln(cond, w_down, w_up, b):
    mod = (jax.nn.silu(cond) @ w_down) @ w_up + b
    return jnp.split(mod, 6, axis=-1)


def _modulate(h, shift, scale):
    return h * (1 + scale[:, None, :]) + shift[:, None, :]


def _axial_rope_angles(n_tokens, dim):
    t = jnp.arange(n_tokens)
    rows = (t // GRID_W).astype(jnp.float32)
    cols = (t % GRID_W).astype(jnp.float32)
    n_pairs = dim // 4
    freqs = ROPE_THETA ** (-jnp.arange(n_pairs, dtype=jnp.float32) / n_pairs)
    return jnp.concatenate([rows[:, None] * freqs, cols[:, None] * freqs], axis=-1)


def _apply_rope(x, ang):
    half = x.shape[-1] // 2
    xf = x.astype(jnp.float32).reshape(x.shape[:-1] + (half, 2))
    bshape = (1, ang.shape[0]) + (1,) * (x.ndim - 3) + (half,)
    cos = jnp.cos(ang).reshape(bshape)
    sin = jnp.sin(ang).reshape(bshape)
    x0, x1 = xf[..., 0], xf[..., 1]
    out = jnp.stack([x0 * cos - x1 * sin, x0 * sin + x1 * cos], axis=-1)
    return out.reshape(x.shape).astype(x.dtype)


def _sweep_query_blocks(fn, q):
    b, s = q.shape[:2]
    nblk = s // Q_BLOCK
    qb = jnp.moveaxis(q.reshape((b, nblk, Q_BLOCK) + q.shape[2:]), 1, 0)
    ob = lax.map(fn, qb)
    return jnp.moveaxis(ob, 0, 1).reshape((b, s) + ob.shape[3:])


def _attend_gqa(q, k, v):
    s = jnp.einsum("bqhgd,bkhd->bhgqk", q, k).astype(jnp.float32) * (q.shape[-1] ** -0.5)
    p = jax.nn.softmax(s, axis=-1).astype(v.dtype)
    return jnp.einsum("bhgqk,bkhd->bqhgd", p, v)


def _attend_diff(q, k, v, lam):
    s = jnp.einsum("bqhcd,bkhcd->bhcqk", q, k).astype(jnp.float32) * (q.shape[-1] ** -0.5)
    p = jax.nn.softmax(s, axis=-1)
    w = (p[:, :, 0] - lam * p[:, :, 1]).astype(v.dtype)
    return jnp.einsum("bhqk,bkhe->bqhe", w, v)


def _neighbourhood_attention(q, k, v, k_ctx, v_ctx, rpb):
    b, s, h, d = q.shape
    rows = s // GRID_W
    kr = min(NA_ROWS, rows)
    c = np.arange(GRID_W)
    c0 = np.clip(c - NA_COLS // 2, 0, GRID_W - NA_COLS)
    col_mask = jnp.asarray((c[None, :] >= c0[:, None]) & (c[None, :] < c0[:, None] + NA_COLS))
    col_idx = jnp.asarray(np.clip(c[None, :] - c[:, None] + NA_COLS - 1, 0, 2 * NA_COLS - 2))
    q_grid = q.reshape(b, rows, GRID_W, h, d)
    k_grid = k.reshape(b, rows, GRID_W, h, d)
    v_grid = v.reshape(b, rows, GRID_W, h, d)
    scale = d ** -0.5

    def row_block(xs):
        r, q_row = xs
        r0 = jnp.clip(r - kr // 2, 0, rows - kr)
        k_blk = lax.dynamic_slice_in_dim(k_grid, r0, kr, axis=1)
        v_blk = lax.dynamic_slice_in_dim(v_grid, r0, kr, axis=1)
        s_loc = jnp.einsum("bqhd,biwhd->bhqiw", q_row, k_blk).astype(jnp.float32) * scale
        dr = r0 + jnp.arange(kr) - r + (NA_ROWS - 1)
        bias = jnp.transpose(rpb[:, dr][:, :, col_idx], (0, 2, 1, 3)).astype(jnp.float32)
        s_loc = jnp.where(col_mask[:, None, :], s_loc + bias, -jnp.inf)
        s_ctx = jnp.einsum("bqhd,bchd->bhqc", q_row, k_ctx).astype(jnp.float32) * scale
        s_all = jnp.concatenate([s_loc.reshape(b, h, GRID_W, kr * GRID_W), s_ctx], axis=-1)
        p = jax.nn.softmax(s_all, axis=-1).astype(v.dtype)
        p_loc = p[..., : kr * GRID_W].reshape(b, h, GRID_W, kr, GRID_W)
        p_ctx = p[..., kr * GRID_W:]
        return (jnp.einsum("bhqiw,biwhd->bqhd", p_loc, v_blk)
                + jnp.einsum("bhqc,bchd->bqhd", p_ctx, v_ctx))

    out = lax.map(row_block, (jnp.arange(rows), jnp.moveaxis(q_grid, 1, 0)))
    return jnp.moveaxis(out, 0, 1).reshape(b, s, h * d)


def _project(h, w):
    b, s = h.shape[:2]
    sizes = [A_Q_W, A_KV_W, A_KV_W, B_W, B_W, B_W, C_QK_W, C_QK_W, C_V_W, GATE_RANK]
    offsets = [int(o) for o in np.cumsum(sizes)[:-1]]
    a_q, a_k, a_v, b_q, b_k, b_v, c_q, c_k, c_v, g_low = jnp.split(h @ w, offsets, axis=-1)
    return (a_q.reshape(b, s, A_KV_HEADS, A_GROUP, HEAD_DIM),
            a_k.reshape(b, s, A_KV_HEADS, HEAD_DIM),
            a_v.reshape(b, s, A_KV_HEADS, HEAD_DIM),
            b_q.reshape(b, s, B_HEADS, HEAD_DIM),
            b_k.reshape(b, s, B_HEADS, HEAD_DIM),
            b_v.reshape(b, s, B_HEADS, HEAD_DIM),
            c_q.reshape(b, s, C_HEADS, 2, C_QK_DIM),
            c_k.reshape(b, s, C_HEADS, 2, C_QK_DIM),
            c_v.reshape(b, s, C_HEADS, C_V_DIM),
            g_low)


def _merge(g_low, o_a, o_b, o_c, w_gate_up, b_gate, w_branch, w_out):
    gates = jax.nn.sigmoid((g_low @ w_gate_up + b_gate).astype(jnp.float32)).astype(o_a.dtype)
    g_a, g_b, g_c = jnp.split(gates, N_BRANCH, axis=-1)
    merged = (g_a * (o_a @ w_branch[:A_Q_W])
              + g_b * (o_b @ w_branch[A_Q_W:A_Q_W + B_W])
              + g_c * (o_c @ w_branch[A_Q_W + B_W:]))
    return merged @ w_out


def _swiglu(h, w_in, w_out):
    gate, up = jnp.split(h @ w_in, 2, axis=-1)
    return (jax.nn.silu(gate) * up) @ w_out


def setup_inputs(seed: int = 0) -> dict:
    key = jax.random.key(seed)
    ks = jax.random.split(key, 26)
    f32 = jnp.float32
    L, D = DEPTH, D_MODEL

    def nrm(k, shape, scale):
        return jax.random.normal(k, shape, f32) * scale

    def gain(k, shape):
        return 1.0 + 0.02 * jax.random.normal(k, shape, f32)

    return {
        "x": nrm(ks[0], (BATCH, SEQ, D), 1.0),
        "c": nrm(ks[1], (BATCH, D), 1.0),
        "ctx": nrm(ks[2], (BATCH, CTX_LEN, D), 1.0),
        "c_ctx": nrm(ks[3], (D,), 1.0),
        "ada_down": nrm(ks[4], (L, D, ADA_RANK), D ** -0.5),
        "ada_up": nrm(ks[5], (L, ADA_RANK, 6 * D), ADA_RANK ** -0.5),
        "ada_bias": nrm(ks[6], (L, 6 * D), 0.02),
        "norm_mix_pre": gain(ks[7], (L, D)),
        "norm_mix_post": gain(ks[8], (L, D)),
        "norm_ffn_pre": gain(ks[9], (L, D)),
        "norm_ffn_post": gain(ks[10], (L, D)),
        "w_in": nrm(ks[11], (L, D, IN_W), D ** -0.5),
        "a_q_norm": gain(ks[12], (L, HEAD_DIM)),
        "a_k_norm": gain(ks[13], (L, HEAD_DIM)),
        "b_rel_bias": nrm(ks[14], (L, B_HEADS, 2 * NA_ROWS - 1, 2 * NA_COLS - 1), 0.1),
        "c_lambda_q1": nrm(ks[15], (L, C_QK_DIM), 0.1),
        "c_lambda_k1": nrm(ks[16], (L, C_QK_DIM), 0.1),
        "c_lambda_q2": nrm(ks[17], (L, C_QK_DIM), 0.1),
        "c_lambda_k2": nrm(ks[18], (L, C_QK_DIM), 0.1),
        "c_subln": gain(ks[19], (L, C_V_DIM)),
        "w_gate_up": nrm(ks[20], (L, GATE_RANK, N_BRANCH * D), GATE_RANK ** -0.5),
        "b_gate": nrm(ks[21], (L, N_BRANCH * D), 0.02),
        "w_branch": nrm(ks[22], (L, MIX_W, D), MIX_W ** -0.5),
        "w_out": nrm(ks[23], (L, D, D), D ** -0.5),
        "w_ffn_in": nrm(ks[24], (L, D, 2 * D_FF), D ** -0.5),
        "w_ffn_out": nrm(ks[25], (L, D_FF, D), D_FF ** -0.5),
    }


def reference(x, c, ctx, c_ctx, ada_down, ada_up, ada_bias, norm_mix_pre, norm_mix_post,
              norm_ffn_pre, norm_ffn_post, w_in, a_q_norm, a_k_norm, b_rel_bias,
              c_lambda_q1, c_lambda_k1, c_lambda_q2, c_lambda_k2, c_subln,
              w_gate_up, b_gate, w_branch, w_out, w_ffn_in, w_ffn_out):
    b, s = x.shape[:2]
    n_ctx = ctx.shape[1]
    ang_a = _axial_rope_angles(s, HEAD_DIM)
    ang_c = _axial_rope_angles(s, C_QK_DIM)
    for l in range(DEPTH):
        last = l == DEPTH - 1
        sh1_x, sc1_x, g1_x, sh2_x, sc2_x, g2_x = _adaln(c, ada_down[l], ada_up[l], ada_bias[l])
        sh1_c, sc1_c, g1_c, sh2_c, sc2_c, g2_c = _adaln(c_ctx[None], ada_down[l], ada_up[l], ada_bias[l])

        h_x = _modulate(_rms_norm(x, norm_mix_pre[l]), sh1_x, sc1_x)
        h_c = _modulate(_rms_norm(ctx, norm_mix_pre[l]), sh1_c, sc1_c)
        aq_x, ak_x, av_x, bq_x, bk_x, bv_x, cq_x, ck_x, cv_x, gl_x = _project(h_x, w_in[l])
        aq_c, ak_c, av_c, bq_c, bk_c, bv_c, cq_c, ck_c, cv_c, gl_c = _project(h_c, w_in[l])

        aq_x = _apply_rope(_rms_norm(aq_x, a_q_norm[l]), ang_a)
        ak_x = _apply_rope(_rms_norm(ak_x, a_k_norm[l]), ang_a)
        aq_c = _rms_norm(aq_c, a_q_norm[l])
        ak_c = _rms_norm(ak_c, a_k_norm[l])
        k_all_a = jnp.concatenate([ak_c, ak_x], axis=1)
        v_all_a = jnp.concatenate([av_c, av_x], axis=1)
        oa_x = _sweep_query_blocks(lambda qb: _attend_gqa(qb, k_all_a, v_all_a), aq_x).reshape(b, s, A_Q_W)

        ob_x = _neighbourhood_attention(bq_x, bk_x, bv_x, bk_c, bv_c, b_rel_bias[l])

        lam_init = 0.8 - 0.6 * math.exp(-0.3 * l)
        lam = (jnp.exp(jnp.sum(c_lambda_q1[l].astype(jnp.float32) * c_lambda_k1[l].astype(jnp.float32)))
               - jnp.exp(jnp.sum(c_lambda_q2[l].astype(jnp.float32) * c_lambda_k2[l].astype(jnp.float32)))
               + lam_init)
        cq_x = _apply_rope(cq_x, ang_c)
        ck_x = _apply_rope(ck_x, ang_c)
        k_all_c = jnp.concatenate([ck_c, ck_x], axis=1)
        v_all_c = jnp.concatenate([cv_c, cv_x], axis=1)
        oc_x = _sweep_query_blocks(lambda qb: _attend_diff(qb, k_all_c, v_all_c, lam), cq_x)
        oc_x = (_rms_norm(oc_x, c_subln[l]) * (1 - lam_init)).reshape(b, s, C_V_W)

        mix_x = _merge(gl_x, oa_x, ob_x, oc_x, w_gate_up[l], b_gate[l], w_branch[l], w_out[l])
        x = x + g1_x[:, None, :] * _rms_norm(mix_x, norm_mix_post[l])

        f_x = _swiglu(_modulate(_rms_norm(x, norm_ffn_pre[l]), sh2_x, sc2_x), w_ffn_in[l], w_ffn_out[l])
        x = x + g2_x[:, None, :] * _rms_norm(f_x, norm_ffn_post[l])

        if not last:
            oa_c = _attend_gqa(aq_c, ak_c, av_c).reshape(b, n_ctx, A_Q_W)
            ob_c = _attend_gqa(bq_c[:, :, :, None], bk_c, bv_c).reshape(b, n_ctx, B_W)
            oc_c = (_rms_norm(_attend_diff(cq_c, ck_c, cv_c, lam), c_subln[l]) * (1 - lam_init)).reshape(b, n_ctx, C_V_W)
            mix_c = _merge(gl_c, oa_c, ob_c, oc_c, w_gate_up[l], b_gate[l], w_branch[l], w_out[l])
            ctx = ctx + g1_c[:, None, :] * _rms_norm(mix_c, norm_mix_post[l])
            f_c = _swiglu(_modulate(_rms_norm(ctx, norm_ffn_pre[l]), sh2_c, sc2_c), w_ffn_in[l], w_ffn_out[l])
            ctx = ctx + g2_c[:, None, :] * _rms_norm(f_c, norm_ffn_post[l])
    return x
```

```python
import contextlib
import math
import numpy as np
import concourse.bass as bass
import concourse.mybir as mybir
from concourse.bass_utils import run_bass_kernel_spmd

F32 = mybir.dt.float32
BF16 = mybir.dt.bfloat16
AF = mybir.ActivationFunctionType
ALU = mybir.AluOpType
AX = mybir.AxisListType

D = 4096
KT = 32
NL = 4096
NCX = 256
T = NL + NCX
DEPTH = 4
IN_W = 9728
D_FF = 11008
EPS = 1e-6
NVEC = 192 + 4 * 32 + 96 + 5


class Phys:
    def __init__(self, i):
        self.i = i
        self.total = 0
        self.sem = None


class Stream:
    def __init__(self, name, step, phys):
        self.name, self.step, self.phys = name, step, phys
        self.base = phys.total
        self.ops = []
        self.n = 0


class Op:
    __slots__ = ("eng", "fn", "deps", "stream", "signal", "count", "seq", "desc")


class Res:
    __slots__ = ("name", "wf", "rf")

    def __init__(self, name="r"):
        self.name = name
        self.wf = {}
        self.rf = {}


ENGS = ("pe", "act", "dve", "pool", "sp")
NPHYS = 80


class Prog:
    def __init__(self):
        self.ops = {e: [] for e in ENGS}
        self.phys = [Phys(i) for i in range(NPHYS + 4)]
        self.estream = {e: Stream(e, 1, self.phys[NPHYS + i]) for i, e in enumerate(("pe", "act", "dve", "pool"))}
        self.free = list(self.phys[:NPHYS - 8])
        self.free_sw = list(self.phys[NPHYS - 8:NPHYS])
        self.live = []
        self.seq = 0
        self.pend = {e: {} for e in ENGS}

    def dstream(self, name, sw=False):
        s = Stream(name, 16, (self.free_sw if sw else self.free).pop())
        s.sw = sw
        self.live.append(s)
        return s

    def barrier(self):
        ops = []
        for s in list(self.estream.values()) + self.live:
            if s.ops:
                o = s.ops[-1]
                o.signal = True
                ops.append(o)
        for s in self.live:
            s.phys.total = s.base + 16 * len(s.ops)
            (self.free_sw if s.sw else self.free).append(s.phys)
        self.live = []
        for e in ENGS:
            pe_ = self.pend[e]
            for o in ops:
                pe_[o.stream] = o

    def emit(self, eng, fn, reads=(), writes=(), stream=None, waw=True):
        op = Op()
        op.eng, op.fn, op.signal, op.count = eng, fn, False, 0
        op.seq = self.seq
        op.desc = getattr(self, "phase", "")
        self.seq += 1
        own = self.estream.get(eng) if stream is None else None
        op.stream = stream if stream is not None else own
        assert op.stream is not None
        deps = {}

        def add(o, raw):
            s = o.stream
            if s is own and (eng == "pe" or not raw):
                return
            cur = deps.get(s)
            if cur is None or cur.seq < o.seq:
                deps[s] = o

        if self.pend[eng]:
            for o in self.pend[eng].values():
                if o.stream is not self.estream.get(eng):
                    add(o, True)
            self.pend[eng] = {}
        for r in reads:
            for o in r.wf.values():
                add(o, True)
        for w in writes:
            if waw == "p":
                for o in w.rf.values():
                    add(o, False)
            elif waw:
                for o in w.wf.values():
                    add(o, True)
                for o in w.rf.values():
                    add(o, False)
        op.deps = list(deps.values())
        for d in op.deps:
            d.signal = True
        for w in writes:
            if waw is True:
                w.wf = {op.stream: op}
                w.rf = {}
            else:
                w.wf[op.stream] = op
        for r in reads:
            r.rf[op.stream] = op
        if stream is not None:
            op.signal = True
        op.stream.ops.append(op)
        self.ops[eng].append(op)
        return op

    def finalize(self, nc, es):
        self.barrier()
        self.emit("sp", None, stream=self.dstream("fin"))
        for s in self.estream.values():
            c = s.base
            for o in s.ops:
                if o.signal:
                    c += 1
                o.count = c
        for e in ENGS:
            for o in self.ops[e]:
                pass
        used = set()
        for e in ENGS:
            for o in self.ops[e]:
                if o.signal:
                    used.add(o.stream.phys)
        for p in sorted(used, key=lambda p: p.i):
            p.sem = es.enter_context(nc.semaphore(f"s{p.i}"))
        block = es.enter_context(nc.Block())

        def replay(name, e):
            waited = {}
            for o in self.ops[name]:
                for d in o.deps:
                    ph = d.stream.phys
                    if waited.get(ph, 0) >= d.count:
                        continue
                    e.wait_ge(ph.sem, d.count)
                    waited[ph] = d.count
                if o.fn is None:
                    continue
                ins = o.fn(e)
                if o.signal:
                    ins.then_inc(o.stream.phys.sem, o.stream.step)

        @block.tensor
        def _(e):
            replay("pe", e)

        @block.scalar
        def _(e):
            replay("act", e)

        @block.vector
        def _(e):
            replay("dve", e)

        @block.gpsimd
        def _(e):
            replay("pool", e)

        @block.sync
        def _(e):
            replay("sp", e)
        return sum(len(v) for v in self.ops.values())

    def dma(self, eng, stream, out, in_, reads=(), writes=(), waw=True):
        op = self.emit(eng, lambda e: e.dma_start(out=out, in_=in_), reads, writes, stream=stream, waw=waw)
        op.count = stream.base + 16 * len(stream.ops)
        return op


import os
SMOKE = bool(os.environ.get("MK_SMOKE"))


def lim(it, n=2):
    it = list(it)
    return it[:n] if SMOKE else it


def chunks(n, c, start=0):
    return [(start + i, min(c, n - i)) for i in range(0, n, c)]


def tok_chunks(c):
    return [(a, b, 0) for a, b in chunks(NL, c)] + [(a, b, 1) for a, b in chunks(NCX, c, NL)]


class Builder:
    def __init__(self, n_layers, debug=()):
        self.nl = n_layers
        self.debug = set(debug)
        self.nc = bass.Bass("TRN2", target_bir_lowering=False)
        self.P = Prog()
        self.es = contextlib.ExitStack()
        self.uid = 0

    def din(self, name, shape, dt=F32):
        return self.nc.dram_tensor(name, list(shape), dt, kind="ExternalInput").ap()

    def dscr(self, name, shape, dt):
        kind = "ExternalOutput" if name in self.debug else "Internal"
        return self.nc.dram_tensor(name, list(shape), dt, kind=kind).ap()

    def areset(self):
        self.aoff = self.amark

    def f32(self, *shape):
        n = int(np.prod(shape))
        a = self.arena[:, self.aoff:self.aoff + n]
        self.aoff += n
        assert self.aoff <= self.asize, ("arena overflow", self.aoff)
        return self._shape(a, shape)

    def bf16(self, *shape):
        n = int(np.prod(shape))
        n2 = (n + 1) // 2
        a = self.arena[:, self.aoff:self.aoff + n2].bitcast(BF16)[:, 0:n]
        self.aoff += n2
        assert self.aoff <= self.asize, ("arena overflow", self.aoff)
        return self._shape(a, shape)

    @staticmethod
    def _shape(a, shape):
        if len(shape) == 1:
            return a
        if len(shape) == 2:
            return a.rearrange("p (a b) -> p a b", b=shape[1])
        return a.rearrange("p (a b c) -> p a b c", b=shape[1], c=shape[2])

    def act(self, out, in_, func, reads, writes, waw=True, **kw):
        return self.P.emit("act", lambda e: e.activation(out=out, in_=in_, func=func, **kw), reads, writes, waw=waw)

    def tt(self, eng, out, in0, in1, op, reads, writes, waw=True):
        return self.P.emit(eng, lambda e: e.tensor_tensor(out=out, in0=in0, in1=in1, op=op), reads, writes, waw=waw)

    def ts(self, eng, out, in0, s1, s2, op0, op1, reads, writes, waw=True):
        if s2 is None:
            return self.P.emit(eng, lambda e: e.tensor_scalar(out=out, in0=in0, scalar1=s1, scalar2=None, op0=op0), reads, writes, waw=waw)
        return self.P.emit(eng, lambda e: e.tensor_scalar(out=out, in0=in0, scalar1=s1, scalar2=s2, op0=op0, op1=op1), reads, writes, waw=waw)

    def stt(self, eng, out, in0, scalar, in1, op0, op1, reads, writes, waw=True):
        return self.P.emit(eng, lambda e: e.scalar_tensor_tensor(out=out, in0=in0, scalar=scalar, in1=in1, op0=op0, op1=op1), reads, writes, waw=waw)

    def cp(self, eng, out, in_, reads, writes, waw=True):
        if eng == "act":
            return self.act(out, in_, AF.Copy, reads, writes, waw=waw)
        return self.P.emit(eng, lambda e: e.tensor_copy(out=out, in_=in_), reads, writes, waw=waw)

    def recip(self, out, in_, reads, writes):
        return self.P.emit("dve", lambda e: e.reciprocal(out=out, in_=in_), reads, writes)

    def mm(self, out, lhsT, rhs, start, stop, reads, writes, waw=True):
        return self.P.emit("pe", lambda e: e.matmul(out, lhsT=lhsT, rhs=rhs, start=start, stop=stop), reads, writes, waw=waw)

    def tr(self, out, in_, ident, reads, writes, waw=True):
        return self.P.emit("pe", lambda e: e.matmul(out, lhsT=in_, rhs=ident, start=True, stop=True, is_transpose=True),
                           reads, writes, waw=waw)

    def rstd_from_sum(self, out, ps, n, reads, writes):
        self.act(out, ps, AF.Sqrt, reads, writes, bias=self.eps_c[:, 0:1], scale=1.0 / n)
        self.recip(out, out, writes, writes)

    def build(self):
        nc, P, es = self.nc, self.P, self.es
        L = self.nl
        I = {}
        I["xin"] = self.din("xin", [T, D])
        I["cT"] = self.din("cT", [128, KT * 2])
        I["ropeA"] = self.din("ropeA", [128, 2, NL])
        I["ropeC"] = self.din("ropeC", [128, 2, NL])
        I["maskB"] = self.din("maskB", [128, 64])
        I["ident"] = self.din("ident", [128, 128])
        I["perm"] = self.din("perm", [128, 128])
        I["ada_down"] = self.din("ada_down", [L, D, 256])
        I["ada_up"] = self.din("ada_up", [L, 256, 6 * D])
        I["vecs"] = self.din("vecs", [L, 128, NVEC])
        I["lamv"] = self.din("lamv", [L, 4 * 64])
        I["bias_tab"] = self.din("bias_tab", [L, 128, 8 * 15 * 64])
        I["w_in"] = self.din("w_in", [L, D, IN_W])
        I["w_gate_up"] = self.din("w_gate_up", [L, 512, 3 * D])
        I["w_branch"] = self.din("w_branch", [L, D, D])
        I["w_out"] = self.din("w_out", [L, D, D])
        I["w_ffn_in"] = self.din("w_ffn_in", [L, D, 2 * D_FF])
        I["w_ffn_out"] = self.din("w_ffn_out", [L, D_FF, D])
        self.I = I
        self.xout = nc.dram_tensor("xout", [T, D], F32, kind="ExternalOutput").ap()
        S = {}
        S["xT"] = self.dscr("xT", [D, T], F32)
        S["yT"] = self.dscr("yT", [D, T], F32)
        S["hT"] = self.dscr("hT", [D, T], BF16)
        S["projT"] = self.dscr("projT", [IN_W, T], BF16)
        S["qkT"] = self.dscr("qkT", [36 * 128, T], BF16)
        S["oT"] = self.dscr("oT", [D, T], BF16)
        S["gatesT"] = self.dscr("gatesT", [3 * D, T], BF16)
        S["mergedT"] = self.dscr("mergedT", [D, T], BF16)
        S["hidT"] = self.dscr("hidT", [D_FF, T], BF16)
        self.S = S
        self.R = {k: Res(k) for k in S}
        self.R["xout"] = Res("xout")

        with es:
            self.asize = 50000
            self.arena = es.enter_context(nc.sbuf_tensor("arena", [128, self.asize], F32))
            self.banks = [es.enter_context(nc.psum_tensor(f"bank{i}", [128, 512], F32)) for i in range(8)]
            self.bres = [Res(f"bank{i}") for i in range(8)]
            self.aoff = 0
            self.amark = 0
            self.rc = Res("consts")
            ident32 = self.f32(128)
            perm32 = self.f32(128)
            self.ident32 = ident32
            self.identb = self.bf16(128)
            self.permb = self.bf16(128)
            self.onesb = self.bf16(128)
            self.eps_c = self.f32(1)
            self.rstdY = self.f32(T)
            self.r_rstdY = Res("rstdY")
            self.vecs = self.f32(NVEC)
            self.modT = self.f32(192, 2)
            self.coef = self.f32(4, KT, 2)
            self.gcprev = self.f32(KT, 2)
            self.lam = self.f32(4)
            self.r_vec = Res("vecs")
            self.amark = self.aoff
            st = P.dstream("const")
            P.dma("sp", st, ident32, I["ident"], writes=[self.rc])
            P.dma("sp", st, perm32, I["perm"], writes=[self.rc], waw=False)
            self.cp("dve", self.identb, ident32, [self.rc], [self.rc], waw=False)
            self.cp("dve", self.permb, perm32, [self.rc], [self.rc], waw=False)
            P.emit("dve", lambda e: e.memset(self.onesb, 1.0), (), [self.rc], waw=False)
            P.emit("dve", lambda e: e.memset(self.eps_c, EPS), (), [self.rc], waw=False)
            P.emit("dve", lambda e: e.memset(self.coef, 0.0), (), [self.rc], waw=False)
            P.barrier()

            self.pending = bool(os.environ.get("MK_PEND"))
            self.phase_init()
            for l in range(L):
                self.layer(l)
            self.phase_final()
            n = P.finalize(nc, es)
        self.n_ops = n
        return nc

    def phase_init(self):
        P = self.P
        P.phase = "phase_init"
        self.areset()
        NB = 2
        xin = [self.f32(D) for _ in range(NB)]
        r_in = [Res() for _ in range(NB)]
        s_in = [P.dstream(f"ini{i}") for i in range(NB)]
        stg = [self.f32(KT, 128) for _ in range(NB)]
        r_st = [Res() for _ in range(NB)]
        s_st = [P.dstream(f"inis{i}") for i in range(NB)]
        bi = 0
        for blk in lim(range(T // 128)):
            s = blk % NB
            P.dma("sp", s_in[s], xin[s], self.I["xin"][blk * 128:(blk + 1) * 128, :], writes=[r_in[s]])
            for k0 in range(0, KT, 4):
                b = bi % 8
                bi += 1
                for k in range(4):
                    self.tr(self.banks[b][:, k * 128:(k + 1) * 128], xin[s][:, (k0 + k) * 128:(k0 + k + 1) * 128],
                            self.ident32, [r_in[s], self.rc], [self.bres[b]], waw=(k == 0))
                eng = "act" if (k0 // 4) % 2 == 0 else "dve"
                self.cp(eng, stg[s][:, k0:k0 + 4, :], self.banks[b][:, :].rearrange("p (a b) -> p a b", b=128),
                        [self.bres[b]], [r_st[s]], waw="p")
            P.dma("sp", s_st[s], self.S["xT"][:, blk * 128:(blk + 1) * 128].rearrange("(kt p) t -> p kt t", p=128),
                  stg[s], reads=[r_st[s]], writes=[self.R["xT"]], waw=False)
        P.barrier()

    def resid_ops(self, kt, xk, yk, rstd, gck, r_x, r_y, n):
        par = 1 if kt % 3 == 2 else 0
        first = kt in (0, 2)
        if par == 0:
            self.tt("dve", yk, yk, rstd, ALU.mult, [r_y[0], self.r_rstdY], [r_y[0]], waw=first)
            self.stt("dve", xk, yk, gck, xk, ALU.mult, ALU.add, [r_y[0], r_x[0], self.r_vec], [r_x[0]], waw=first)
        else:
            self.tt("pool", yk, yk, rstd, ALU.mult, [r_y[1], self.r_rstdY], [r_y[1]], waw=first)
            self.tt("pool", yk, yk, gck.broadcast_to([128, n]), ALU.mult, [r_y[1], self.r_vec], [r_y[1]], waw=False)
            self.tt("pool", xk, xk, yk, ALU.add, [r_y[1], r_x[1]], [r_x[1]], waw=first)
        return par


    def phase_final(self):
        P = self.P
        P.phase = "phase_final"
        self.areset()
        NB = 2
        xs = [self.f32(KT, 128) for _ in range(NB)]
        ys = [self.f32(KT, 128) for _ in range(NB)]
        r_x = [[Res(), Res()] for _ in range(NB)]
        r_y = [[Res(), Res()] for _ in range(NB)]
        s_x = [P.dstream(f"fx{i}") for i in range(NB)]
        s_y = [P.dstream(f"fy{i}") for i in range(NB)]
        og = [self.f32(D) for _ in range(NB)]
        r_o = [Res() for _ in range(NB)]
        s_o = [P.dstream(f"fo{i}") for i in range(NB)]
        gc = self.coef[:, 3]
        bi = 0
        for blk in lim(range(T // 128)):
            s = blk % NB
            col = 0 if blk * 128 < NL else 1
            tsl = slice(blk * 128, (blk + 1) * 128)
            P.dma("sp", s_x[s], xs[s], self.S["xT"][:, tsl].rearrange("(kt p) t -> p kt t", p=128),
                  reads=[self.R["xT"]], writes=r_x[s])
            if self.pending:
                P.dma("sp", s_y[s], ys[s], self.S["yT"][:, tsl].rearrange("(kt p) t -> p kt t", p=128),
                      reads=[self.R["yT"]], writes=r_y[s])
                for kt in range(KT):
                    self.resid_ops(kt, xs[s][:, kt, :], ys[s][:, kt, :], self.rstdY[:, tsl], gc[:, kt, col:col + 1],
                                   r_x[s], r_y[s], 128)
            for k0 in range(0, KT, 4):
                b = bi % 8
                bi += 1
                for k in range(4):
                    self.tr(self.banks[b][:, k * 128:(k + 1) * 128], xs[s][:, k0 + k, :], self.ident32,
                            r_x[s] + [self.rc], [self.bres[b]], waw=(k == 0))
                eng = "act" if (k0 // 4) % 2 == 0 else "dve"
                self.cp(eng, og[s][:, k0 * 128:(k0 + 4) * 128], self.banks[b][:, :], [self.bres[b]], [r_o[s]], waw="p")
            P.dma("sp", s_o[s], self.xout[tsl, :], og[s], reads=[r_o[s]], writes=[self.R["xout"]], waw=False)
        P.barrier()

    def layer(self, l):
        I, S = self.I, self.S
        W = {k: I[k][l] for k in ("ada_down", "ada_up", "vecs", "lamv", "bias_tab", "w_in", "w_gate_up",
                                  "w_branch", "w_out", "w_ffn_in", "w_ffn_out")}
        phases = [lambda: self.phase_mod(W), lambda: self.phase_rn(0), lambda: self.phase_proj(W), self.phase_qkprep,
                  lambda: self.phase_att_ac(W), lambda: self.phase_att_b(W), lambda: self.phase_gates(W),
                  lambda: self.phase_branch(W),
                  lambda: self.phase_outproj(W["w_out"], S["mergedT"], self.R["mergedT"], D, 1088),
                  lambda: self.phase_rn(1), lambda: self.phase_ffn_in(W),
                  lambda: self.phase_outproj(W["w_ffn_out"], S["hidT"], self.R["hidT"], D_FF, 544)]
        sel = os.environ.get("MK_PHASES")
        for i, ph in enumerate(phases):
            if sel is None or str(i) in sel.split(","):
                ph()

    V_BIAS, V_GPRE, V_GPOST, V_GFPRE, V_GFPOST, V_BGATE, V_AQ, V_AK, V_SUBLN, V_LI, V_1MLI = (
        0, 192, 224, 256, 288, 320, 416, 417, 418, 419, 420)

    def phase_mod(self, W):
        P = self.P
        P.phase = "phase_mod"
        self.areset()
        st = P.dstream("mod")
        r = Res("modin")
        cs = self.f32(KT, 2)
        sg = self.f32(KT, 2)
        dn = self.f32(KT, 256)
        lv = self.f32(4, 64)
        P.dma("sp", st, cs, self.I["cT"].rearrange("p (a b) -> p a b", b=2), writes=[r])
        P.dma("sp", st, dn, W["ada_down"].rearrange("(kt p) n -> p kt n", p=128), writes=[r], waw=False)
        P.dma("sp", st, self.vecs, W["vecs"], writes=[r, self.r_vec], waw=False)
        P.dma("sp", st, lv, W["lamv"].partition_broadcast(128).rearrange("p (a b) -> p a b", b=64), writes=[r], waw=False)
        self.act(sg, cs, AF.Sigmoid, [r], [r], waw=False)
        self.tt("dve", cs, cs, sg, ALU.mult, [r], [r], waw=False)
        dT = self.f32(2, 2)
        rd = Res("dT")
        for rt in range(2):
            b = rt
            for kt in range(KT):
                self.mm(self.banks[b][:, 0:2], dn[:, kt, rt * 128:(rt + 1) * 128], cs[:, kt, :], kt == 0, kt == KT - 1,
                        [r], [self.bres[b]], waw=(kt == 0))
            self.cp("dve", dT[:, rt, :], self.banks[b][:, 0:2], [self.bres[b]], [rd], waw=False)
        NCH = 2048
        ups = [self.f32(2, NCH) for _ in range(2)]
        r_up = [Res() for _ in range(2)]
        s_up = [P.dstream(f"up{i}") for i in range(2)]
        rm = self.r_vec
        for ch in range(6 * D // NCH):
            s = ch % 2
            b = 2 + ch % 2
            P.dma("sp", s_up[s], ups[s], W["ada_up"][:, ch * NCH:(ch + 1) * NCH].rearrange("(rt p) n -> p rt n", p=128),
                  writes=[r_up[s]])
            for i in range(NCH // 128):
                for rt in range(2):
                    self.mm(self.banks[b][:, 2 * i:2 * i + 2], ups[s][:, rt, i * 128:(i + 1) * 128], dT[:, rt, :],
                            rt == 0, rt == 1, [r_up[s], rd], [self.bres[b]], waw=(i == 0 and rt == 0))
            j0 = ch * (NCH // 128)
            nj = NCH // 128
            self.tt("dve", self.modT[:, j0:j0 + nj, :], self.banks[b][:, 0:2 * nj].rearrange("p (a b) -> p a b", b=2),
                    self.vecs[:, self.V_BIAS + j0:self.V_BIAS + j0 + nj].unsqueeze(2).broadcast_to([128, nj, 2]), ALU.add,
                    [self.bres[b], r], [rm], waw=False)
        self.cp("dve", self.gcprev, self.coef[:, 3], [rm], [rm], waw=False)
        m = self.modT
        v = self.vecs

        def gv(off):
            return v[:, off:off + KT].unsqueeze(2).broadcast_to([128, KT, 2])
        self.ts("dve", self.coef[:, 0], m[:, 32:64, :], 1.0, None, ALU.add, None, [rm], [rm], waw=False)
        self.tt("dve", self.coef[:, 0], self.coef[:, 0], gv(self.V_GPRE), ALU.mult, [rm], [rm], waw=False)
        self.tt("dve", self.coef[:, 1], m[:, 64:96, :], gv(self.V_GPOST), ALU.mult, [rm], [rm], waw=False)
        self.ts("dve", self.coef[:, 2], m[:, 128:160, :], 1.0, None, ALU.add, None, [rm], [rm], waw=False)
        self.tt("dve", self.coef[:, 2], self.coef[:, 2], gv(self.V_GFPRE), ALU.mult, [rm], [rm], waw=False)
        self.tt("dve", self.coef[:, 3], m[:, 160:192, :], gv(self.V_GFPOST), ALU.mult, [rm], [rm], waw=False)
        pr = self.f32(2, 64)
        sm = self.f32(2)
        self.tt("dve", pr[:, 0, :], lv[:, 0, :], lv[:, 1, :], ALU.mult, [r], [rm], waw=False)
        self.tt("dve", pr[:, 1, :], lv[:, 2, :], lv[:, 3, :], ALU.mult, [r], [rm], waw=False)
        P.emit("dve", lambda e: e.reduce_sum(out=sm[:, 0:1], in_=pr[:, 0, :], axis=AX.X), [rm], [rm], waw=False)
        P.emit("dve", lambda e: e.reduce_sum(out=sm[:, 1:2], in_=pr[:, 1, :], axis=AX.X), [rm], [rm], waw=False)
        self.act(sm, sm, AF.Exp, [rm], [rm], waw=False)
        self.tt("dve", self.lam[:, 2:3], sm[:, 1:2], sm[:, 0:1], ALU.subtract, [rm], [rm], waw=False)
        self.tt("dve", self.lam[:, 0:1], self.lam[:, 2:3], v[:, self.V_LI:self.V_LI + 1], ALU.subtract, [rm], [rm], waw=False)
        self.tt("dve", self.lam[:, 1:2], v[:, self.V_SUBLN:self.V_SUBLN + 1], v[:, self.V_1MLI:self.V_1MLI + 1], ALU.mult,
                [rm], [rm], waw=False)
        P.barrier()

    def phase_rn(self, which):
        P = self.P
        P.phase = "phase_rn"
        self.areset()
        CH = 256
        a = self.coef[:, 0] if which == 0 else self.coef[:, 2]
        bofs = 0 if which == 0 else 96
        gc = self.gcprev if which == 0 else self.coef[:, 1]
        xs = self.f32(KT, CH)
        ys = self.f32(KT, CH)
        sq = self.bf16(KT, CH)
        hs = self.bf16(KT, CH)
        rs = self.f32(CH)
        r_x, r_y, r_h = [Res(), Res()], [Res(), Res()], [Res(), Res()]
        r_sq, r_rs = Res(), Res()
        s_x, s_y, s_xo, s_h = P.dstream("rnx"), P.dstream("rny"), P.dstream("rnxo"), P.dstream("rnh")
        zb = 0
        for (t0, cn, col) in lim(tok_chunks(CH)):
            tsl = slice(t0, t0 + cn)
            P.dma("sp", s_x, xs[:, :, 0:cn], self.S["xT"][:, tsl].rearrange("(kt p) t -> p kt t", p=128),
                  reads=[self.R["xT"]], writes=r_x)
            if self.pending:
                P.dma("sp", s_y, ys[:, :, 0:cn], self.S["yT"][:, tsl].rearrange("(kt p) t -> p kt t", p=128),
                      reads=[self.R["yT"]], writes=r_y)
                for kt in range(KT):
                    self.resid_ops(kt, xs[:, kt, 0:cn], ys[:, kt, 0:cn], self.rstdY[:, tsl], gc[:, kt, col:col + 1],
                                   r_x, r_y, cn)
                P.dma("sp", s_xo, self.S["xT"][:, tsl].rearrange("(kt p) t -> p kt t", p=128), xs[:, :, 0:cn],
                      reads=r_x, writes=[self.R["xT"]], waw=False)
            for kt in range(KT):
                self.act(sq[:, kt, 0:cn], xs[:, kt, 0:cn], AF.Square, [r_x[1 if kt % 3 == 2 else 0]], [r_sq], waw=(kt == 0))
            b = zb % 2
            zb += 1
            for kt in range(KT):
                self.mm(self.banks[b][:, 0:cn], self.onesb, sq[:, kt, 0:cn], kt == 0, kt == KT - 1,
                        [r_sq, self.rc], [self.bres[b]], waw=(kt == 0))
            self.rstd_from_sum(rs[:, 0:cn], self.banks[b][:, 0:cn], D, [self.bres[b], self.rc], [r_rs])
            for kt in range(KT):
                par = 1 if kt % 3 == 2 else 0
                first = kt in (0, 2)
                ak, bk = a[:, kt, col:col + 1], self.modT[:, bofs + kt, col:col + 1]
                if par == 0:
                    self.tt("dve", ys[:, kt, 0:cn], xs[:, kt, 0:cn], rs[:, 0:cn], ALU.mult, [r_x[0], r_rs], [r_y[0]], waw=first)
                    self.ts("dve", hs[:, kt, 0:cn], ys[:, kt, 0:cn], ak, bk, ALU.mult, ALU.add, [r_y[0], self.r_vec], [r_h[0]], waw=first)
                else:
                    self.tt("pool", ys[:, kt, 0:cn], xs[:, kt, 0:cn], rs[:, 0:cn], ALU.mult, [r_x[1], r_rs], [r_y[1]], waw=first)
                    self.tt("pool", ys[:, kt, 0:cn], ys[:, kt, 0:cn], ak.broadcast_to([128, cn]), ALU.mult,
                            [r_y[1], self.r_vec], [r_y[1]], waw=False)
                    self.tt("pool", hs[:, kt, 0:cn], ys[:, kt, 0:cn], bk.broadcast_to([128, cn]), ALU.add,
                            [r_y[1], self.r_vec], [r_h[1]], waw=first)
            P.dma("sp", s_h, self.S["hT"][:, tsl].rearrange("(kt p) t -> p kt t", p=128), hs[:, :, 0:cn],
                  reads=r_h, writes=[self.R["hT"]], waw=False)
        self.pending = False
        P.barrier()

    def linear(self, name, srcs, src_res, groups_of, n_tiles, TB, NG, epilogue, banks, flush=None):
        P = self.P
        KTs = [s.shape[0] // 128 for s in srcs]
        blks = [self.bf16(kt, TB) for kt in KTs]
        blk_res = [Res() for _ in srcs]
        blk_str = [P.dstream(f"{name}b{i}") for i in range(len(srcs))]
        g0 = groups_of(0, NG)
        ktot = sum(g[0].shape[0] // 128 for g in g0)
        NW = 2
        wsl = [self.bf16(ktot, NG * 128) for _ in range(NW)]
        w_res = [Res() for _ in range(NW)]
        w_str = [P.dstream(f"{name}w{i}", sw=True) for i in range(NW)]
        bi = 0
        gi = 0
        for (t0, tbn) in lim(chunks(T, TB), 1):
            for i, s in enumerate(srcs):
                P.dma("sp", blk_str[i], blks[i][:, :, 0:tbn], s[:, t0:t0 + tbn].rearrange("(kt p) t -> p kt t", p=128),
                      reads=src_res, writes=[blk_res[i]])
            j0s = lim(range(0, n_tiles, NG))
            last_j = min(j0s[-1] + NG, n_tiles) - 1
            for j0 in j0s:
                ng = min(NG, n_tiles - j0)
                groups = groups_of(j0, ng)
                sl = gi % NW
                gi += 1
                off = 0
                offs = []
                for (Wap, si, kt0) in groups:
                    ktg = Wap.shape[0] // 128
                    P.dma("pool", w_str[sl], wsl[sl][:, off:off + ktg, 0:ng * 128],
                          Wap.rearrange("(kt p) n -> p kt n", p=128), writes=[w_res[sl]], waw=(off == 0))
                    offs.append(off)
                    off += ktg
                for jj in range(ng):
                    j = j0 + jj
                    for ci, (c0, cn) in enumerate(chunks(tbn, 512)):
                        pss, prs = [], []
                        for gidx, (Wap, si, kt0) in enumerate(groups):
                            ktg = Wap.shape[0] // 128
                            b = banks[bi % len(banks)]
                            bi += 1
                            for kt in range(ktg):
                                self.mm(self.banks[b][:, 0:cn], wsl[sl][:, offs[gidx] + kt, jj * 128:(jj + 1) * 128],
                                        blks[si][:, kt0 + kt, c0:c0 + cn], kt == 0, kt == ktg - 1,
                                        [w_res[sl], blk_res[si]], [self.bres[b]], waw=(kt == 0))
                            pss.append(self.banks[b])
                            prs.append(self.bres[b])
                        epilogue(j, t0, c0, cn, ci, pss, prs, j == last_j)
            if flush is not None:
                flush()

    def store_stage(self, name, n, dt, width=512):
        P = self.P
        mk = self.bf16 if dt == BF16 else self.f32
        st = {"buf": [mk(width) for _ in range(n)], "res": [Res() for _ in range(n)],
              "str": [P.dstream(f"{name}{i}") for i in range(n)], "i": 0, "n": n}
        return st

    def phase_proj(self, W):
        P = self.P
        P.phase = "phase_proj"
        self.areset()
        st = self.store_stage("pj", 4, BF16)
        dst, dres = self.S["projT"], self.R["projT"]

        def epi(j, t0, c0, cn, ci, pss, prs, last):
            s = st["i"] % st["n"]
            st["i"] += 1
            self.cp("act" if s % 2 == 0 else "dve", st["buf"][s][:, 0:cn], pss[0][:, 0:cn], [prs[0]], [st["res"][s]])
            P.dma("sp", st["str"][s], dst[j * 128:(j + 1) * 128, t0 + c0:t0 + c0 + cn], st["buf"][s][:, 0:cn],
                  reads=[st["res"][s]], writes=[dres], waw=False)

        self.linear("pj", [self.S["hT"]], [self.R["hT"]],
                    lambda j0, ng: [(W["w_in"][:, j0 * 128:(j0 + ng) * 128], 0, 0)],
                    IN_W // 128, 1088, 2, epi, list(range(8)))
        P.barrier()

    def phase_gates(self, W):
        P = self.P
        P.phase = "phase_gates"
        self.areset()
        st = self.store_stage("gt", 4, BF16)
        dst, dres = self.S["gatesT"], self.R["gatesT"]

        def epi(j, t0, c0, cn, ci, pss, prs, last):
            s = st["i"] % st["n"]
            st["i"] += 1
            self.act(st["buf"][s][:, 0:cn], pss[0][:, 0:cn], AF.Sigmoid, [prs[0], self.r_vec], [st["res"][s]],
                     bias=self.vecs[:, self.V_BGATE + j:self.V_BGATE + j + 1], scale=1.0)
            P.dma("sp", st["str"][s], dst[j * 128:(j + 1) * 128, t0 + c0:t0 + c0 + cn], st["buf"][s][:, 0:cn],
                  reads=[st["res"][s]], writes=[dres], waw=False)

        self.linear("gt", [self.S["projT"][9216:9728, :]], [self.R["projT"]],
                    lambda j0, ng: [(W["w_gate_up"][:, j0 * 128:(j0 + ng) * 128], 0, 0)],
                    3 * D // 128, 2176, 4, epi, list(range(8)))
        P.barrier()

    def phase_branch(self, W):
        P = self.P
        P.phase = "phase_branch"
        self.areset()
        st = self.store_stage("br", 3, BF16)
        NGT = 3
        gts = [[self.bf16(512) for _ in range(3)] for _ in range(NGT)]
        g_res = [Res() for _ in range(NGT)]
        g_str = [P.dstream(f"brg{i}") for i in range(NGT)]
        tmp = [[self.f32(512) for _ in range(3)] for _ in range(NGT)]
        t_res = [Res() for _ in range(NGT)]
        dst, dres = self.S["mergedT"], self.R["mergedT"]
        G = self.S["gatesT"]
        cnt = [0]

        def epi(j, t0, c0, cn, ci, pss, prs, last):
            q = cnt[0] % NGT
            cnt[0] += 1
            for k in range(3):
                P.dma("sp", g_str[q], gts[q][k][:, 0:cn], G[k * D + j * 128:k * D + (j + 1) * 128, t0 + c0:t0 + c0 + cn],
                      reads=[self.R["gatesT"]], writes=[g_res[q]], waw=(k == 0))
            for k in range(3):
                self.tt("dve", tmp[q][k][:, 0:cn], pss[k][:, 0:cn], gts[q][k][:, 0:cn], ALU.mult,
                        [prs[k], g_res[q]], [t_res[q]], waw=(k == 0))
            self.tt("pool", tmp[q][0][:, 0:cn], tmp[q][0][:, 0:cn], tmp[q][1][:, 0:cn], ALU.add, [t_res[q]], [t_res[q]])
            s = st["i"] % st["n"]
            st["i"] += 1
            self.tt("pool", st["buf"][s][:, 0:cn], tmp[q][0][:, 0:cn], tmp[q][2][:, 0:cn], ALU.add, [t_res[q]], [st["res"][s]])
            P.dma("sp", st["str"][s], dst[j * 128:(j + 1) * 128, t0 + c0:t0 + c0 + cn], st["buf"][s][:, 0:cn],
                  reads=[st["res"][s]], writes=[dres], waw=False)

        wb = W["w_branch"]

        def groups(j0, ng):
            cs = slice(j0 * 128, (j0 + ng) * 128)
            return [(wb[0:2048, cs], 0, 0), (wb[2048:3072, cs], 0, 16), (wb[3072:4096, cs], 0, 24)]

        self.linear("br", [self.S["oT"]], [self.R["oT"]], groups, D // 128, 1088, 2, epi, list(range(6)))
        P.barrier()

    def phase_outproj(self, Wap, src, src_res, K, TB):
        P = self.P
        P.phase = "phase_outproj"
        self.areset()
        st = self.store_stage("op", 4, BF16 if os.environ.get("MK_X3") else F32)
        NSQ = 4
        sqs = [self.bf16(512) for _ in range(NSQ)]
        sq_res = [Res() for _ in range(NSQ)]
        dst, dres = self.S["yT"], self.R["yT"]
        if os.environ.get("MK_X3"):
            dst, dres = self.S["oT"], self.R["oT"]
        zb = [5, 6, 7]
        pend = []
        cnt = [0]
        rr = self.r_rstdY

        def do_pend():
            while pend:
                pend.pop(0)()

        def epi(j, t0, c0, cn, ci, pss, prs, last):
            do_pend()
            s = st["i"] % st["n"]
            st["i"] += 1
            q = cnt[0] % NSQ
            cnt[0] += 1
            self.cp("dve", st["buf"][s][:, 0:cn], pss[0][:, 0:cn], [prs[0]], [st["res"][s]])
            self.act(sqs[q][:, 0:cn], st["buf"][s][:, 0:cn], AF.Square, [st["res"][s]], [sq_res[q]])
            P.dma("sp", st["str"][s], dst[j * 128:(j + 1) * 128, t0 + c0:t0 + c0 + cn], st["buf"][s][:, 0:cn],
                  reads=[st["res"][s]], writes=[dres], waw=False)
            zbank = zb[ci]

            def later():
                self.mm(self.banks[zbank][:, 0:cn], self.onesb, sqs[q][:, 0:cn], j == 0, last,
                        [sq_res[q], self.rc], [self.bres[zbank]], waw=(j == 0))
                if last:
                    self.rstd_from_sum(self.rstdY[:, t0 + c0:t0 + c0 + cn], self.banks[zbank][:, 0:cn], D,
                                       [self.bres[zbank], self.rc], [rr])
            if not os.environ.get("MK_X1"):
                pend.append(later)

        NG = 2 if K <= D else 1
        self.linear("op", [src], [src_res], lambda j0, ng: [(Wap[:, j0 * 128:(j0 + ng) * 128], 0, 0)],
                    D // 128, TB, NG, epi, list(range(8)) if os.environ.get("MK_X4") else [0, 1, 2, 3, 4], flush=do_pend)
        self.pending = True
        P.barrier()

    def phase_ffn_in(self, W):
        P = self.P
        P.phase = "phase_ffn_in"
        self.areset()
        st = self.store_stage("fi", 4, BF16)
        NTM = 3
        tmp = [self.f32(512) for _ in range(NTM)]
        t_res = [Res() for _ in range(NTM)]
        dst, dres = self.S["hidT"], self.R["hidT"]
        cnt = [0]

        def epi(j, t0, c0, cn, ci, pss, prs, last):
            q = cnt[0] % NTM
            cnt[0] += 1
            s = st["i"] % st["n"]
            st["i"] += 1
            self.act(tmp[q][:, 0:cn], pss[0][:, 0:cn], AF.Silu, [prs[0]], [t_res[q]])
            self.tt("dve", st["buf"][s][:, 0:cn], pss[1][:, 0:cn], tmp[q][:, 0:cn], ALU.mult, [prs[1], t_res[q]], [st["res"][s]])
            P.dma("sp", st["str"][s], dst[j * 128:(j + 1) * 128, t0 + c0:t0 + c0 + cn], st["buf"][s][:, 0:cn],
                  reads=[st["res"][s]], writes=[dres], waw=False)

        wf = W["w_ffn_in"]

        def groups(j0, ng):
            return [(wf[:, j0 * 128:(j0 + ng) * 128], 0, 0), (wf[:, D_FF + j0 * 128:D_FF + (j0 + ng) * 128], 0, 0)]

        self.linear("fi", [self.S["hT"]], [self.R["hT"]], groups, D_FF // 128, 1088, 2, epi, list(range(8)))
        P.barrier()

    def phase_qkprep(self):
        P = self.P
        P.phase = "phase_qkprep"
        self.areset()
        src, dst = self.S["projT"], self.S["qkT"]
        tiles = [(i, i * 128, 0) for i in range(16)] + [(16 + i, 2048 + i * 128, 1) for i in range(4)]
        tiles += [(20 + i, 6144 + i * 128, 2) for i in range(8)] + [(28 + i, 7168 + i * 128, 2) for i in range(8)]
        NB = 3
        qin = [self.bf16(512) for _ in range(NB)]
        r_q = [Res() for _ in range(NB)]
        s_q = [P.dstream(f"qk{i}") for i in range(NB)]
        sq = [self.bf16(512) for _ in range(NB)]
        r_sq = [Res() for _ in range(NB)]
        rs = [self.f32(512) for _ in range(NB)]
        r_rs = [Res() for _ in range(NB)]
        qn = [self.f32(512) for _ in range(NB)]
        qnb = [self.bf16(512) for _ in range(NB)]
        r_qn = [Res() for _ in range(NB)]
        t1 = [self.f32(512) for _ in range(NB)]
        t2 = [self.f32(512) for _ in range(NB)]
        r_t = [Res() for _ in range(NB)]
        ob = [self.bf16(512) for _ in range(NB)]
        r_o = [Res() for _ in range(NB)]
        s_o = [P.dstream(f"qko{i}") for i in range(NB)]
        rope = {0: self.I["ropeA"], 2: self.I["ropeC"]}
        tab = self.f32(2, 2, 512)
        r_tab = Res()
        s_tab = P.dstream("qktab")
        it = 0
        bi = 0
        for (c0, cn, col) in lim(tok_chunks(512)[-2:] if SMOKE else tok_chunks(512)):
            if col == 0:
                P.dma("sp", s_tab, tab[:, 0, :, 0:cn], rope[0][:, :, c0:c0 + cn], writes=[r_tab])
                P.dma("sp", s_tab, tab[:, 1, :, 0:cn], rope[2][:, :, c0:c0 + cn], writes=[r_tab], waw=False)
            for (dt_, row0, kind) in (tiles[0:1] + tiles[16:17] + tiles[20:21] if SMOKE else tiles):
                s = it % NB
                it += 1
                P.dma("sp", s_q[s], qin[s][:, 0:cn], src[row0:row0 + 128, c0:c0 + cn], reads=[self.R["projT"]], writes=[r_q[s]])
                if kind < 2:
                    self.act(sq[s][:, 0:cn], qin[s][:, 0:cn], AF.Square, [r_q[s]], [r_sq[s]])
                    b = bi % 8
                    bi += 1
                    self.mm(self.banks[b][:, 0:cn], self.onesb, sq[s][:, 0:cn], True, True, [r_sq[s], self.rc], [self.bres[b]])
                    self.rstd_from_sum(rs[s][:, 0:cn], self.banks[b][:, 0:cn], 128, [self.bres[b], self.rc], [r_rs[s]])
                    gcol = self.V_AQ if kind == 0 else self.V_AK
                    dest = qnb[s] if col == 0 else ob[s]
                    dres = r_qn[s] if col == 0 else r_o[s]
                    self.stt("dve", dest[:, 0:cn], qin[s][:, 0:cn], self.vecs[:, gcol:gcol + 1], rs[s][:, 0:cn],
                             ALU.mult, ALU.mult, [r_q[s], r_rs[s], self.r_vec], [dres])
                    xin_, xres = qnb[s], r_qn[s]
                else:
                    xin_, xres = qin[s], r_q[s]
                    if col == 1:
                        self.cp("pool", ob[s][:, 0:cn], qin[s][:, 0:cn], [r_q[s]], [r_o[s]])
                if col == 0:
                    w = 0 if kind < 2 else 1
                    b = bi % 8
                    bi += 1
                    self.mm(self.banks[b][:, 0:cn], self.permb, xin_[:, 0:cn], True, True, [xres, self.rc], [self.bres[b]])
                    self.tt("pool", t1[s][:, 0:cn], xin_[:, 0:cn], tab[:, w, 0, 0:cn], ALU.mult, [xres, r_tab], [r_t[s]])
                    self.tt("dve", t2[s][:, 0:cn], self.banks[b][:, 0:cn], tab[:, w, 1, 0:cn], ALU.mult,
                            [self.bres[b], r_tab], [r_t[s]], waw="p")
                    self.tt("pool", ob[s][:, 0:cn], t1[s][:, 0:cn], t2[s][:, 0:cn], ALU.add, [r_t[s]], [r_o[s]])
                P.dma("sp", s_o[s], dst[dt_ * 128:(dt_ + 1) * 128, c0:c0 + cn], ob[s][:, 0:cn],
                      reads=[r_o[s]], writes=[self.R["qkT"]], waw=False)
        P.barrier()

    def load_kv(self, kT_src, kres, vT_src, vres, KTt, VTt, Vt, r_k, r_v, s_k, s_v, bank_ids):
        P = self.P
        P.dma("sp", s_k, KTt, kT_src, reads=[kres], writes=[r_k])
        P.dma("sp", s_v, VTt, vT_src, reads=[vres], writes=[r_v["t"]])
        nb = T // 128
        for k0 in range(0, nb, 8):
            n = min(8, nb - k0)
            b = bank_ids[(k0 // 8) % len(bank_ids)]
            pb = self.banks[b][:, :].bitcast(BF16)
            for k in range(n):
                self.tr(pb[:, k * 128:(k + 1) * 128], VTt[:, (k0 + k) * 128:(k0 + k + 1) * 128], self.identb,
                        [r_v["t"], self.rc], [self.bres[b]], waw=(k == 0))
            self.cp("dve" if (k0 // 8) % 2 == 0 else "act", Vt[:, k0:k0 + n, :],
                    pb[:, 0:n * 128].rearrange("p (a b) -> p a b", b=128), [self.bres[b]], [r_v["v"]], waw="p")

    def phase_att_ac(self, W):
        P = self.P
        P.phase = "phase_att_ac"
        self.areset()
        qk, pj = self.S["qkT"], self.S["projT"]
        rqk, rpj = self.R["qkT"], self.R["projT"]
        KTt = [self.bf16(T) for _ in range(2)]
        VTt = [self.bf16(T) for _ in range(2)]
        Vt = [self.bf16(T // 128, 128) for _ in range(2)]
        r_k = [Res() for _ in range(2)]
        r_v = [{"t": Res(), "v": Res()} for _ in range(2)]
        s_k = [P.dstream(f"ak{i}") for i in range(2)]
        s_v = [P.dstream(f"av{i}") for i in range(2)]
        NQ = 2
        Qt = [self.bf16(512) for _ in range(NQ)]
        r_qt = [Res() for _ in range(NQ)]
        s_qt = [P.dstream(f"aq{i}") for i in range(NQ)]
        NP = 4
        Pt = [self.bf16(512) for _ in range(NP)]
        r_p = [Res() for _ in range(NP)]
        rz = [self.f32(512) for _ in range(2)]
        r_rz = [Res() for _ in range(2)]
        to = [self.f32(512) for _ in range(3)]
        r_to = Res()
        sqb = self.bf16(512)
        r_sqb = Res()
        rso = self.f32(512)
        r_rso = Res()
        ost = self.store_stage("ao", 2, BF16)
        oT, roT = self.S["oT"], self.R["oT"]
        nkb = T // 128
        qchunks = tok_chunks(512)
        cnt = {"q": 0, "p": 0, "kv": 0}
        SB = [0, 1, 2, 3]
        sbi = [0]

        def attend(qrow0, orow0, kv, scale, comps):
            nc_ = len(comps)
            for (c0, cn, col) in (qchunks[-2:] if SMOKE else qchunks):
                qs = cnt["q"] % NQ
                cnt["q"] += 1
                P.dma("sp", s_qt[qs], Qt[qs][:, 0:cn], qk[qrow0:qrow0 + 128, c0:c0 + cn], reads=[rqk], writes=[r_qt[qs]])
                kbs = list(range(nkb)) if col == 0 else list(range(NL // 128, nkb))
                if nc_ == 1:
                    ob, zb = [4 + cnt["q"] % 2], [6 + cnt["q"] % 2]
                else:
                    ob, zb = [4, 5], [6, 7]
                kbs = lim(kbs, 3)
                for ki, kb in enumerate(kbs):
                    first, last = ki == 0, ki == len(kbs) - 1
                    for ci_, (lo, hi) in enumerate(comps):
                        b = SB[sbi[0] % 4]
                        sbi[0] += 1
                        self.mm(self.banks[b][:, 0:cn], KTt[kv][lo:hi, kb * 128:(kb + 1) * 128], Qt[qs][lo:hi, 0:cn],
                                True, True, [r_k[kv], r_qt[qs]], [self.bres[b]])
                        ps = cnt["p"] % NP
                        cnt["p"] += 1
                        self.act(Pt[ps][:, 0:cn], self.banks[b][:, 0:cn], AF.Exp, [self.bres[b]], [r_p[ps]], scale=scale)
                        self.mm(self.banks[ob[ci_]][:, 0:cn], Vt[kv][:, kb, :], Pt[ps][:, 0:cn], first, last,
                                [r_v[kv]["v"], r_p[ps]], [self.bres[ob[ci_]]], waw=first)
                        self.mm(self.banks[zb[ci_]][:, 0:cn], self.onesb, Pt[ps][:, 0:cn], first, last,
                                [r_p[ps], self.rc], [self.bres[zb[ci_]]], waw=first)
                s = ost["i"] % ost["n"]
                ost["i"] += 1
                for ci_ in range(nc_):
                    self.recip(rz[ci_][:, 0:cn], self.banks[zb[ci_]][:, 0:cn], [self.bres[zb[ci_]]], [r_rz[ci_]])
                if nc_ == 1:
                    self.tt("dve", ost["buf"][s][:, 0:cn], self.banks[ob[0]][:, 0:cn], rz[0][:, 0:cn], ALU.mult,
                            [self.bres[ob[0]], r_rz[0]], [ost["res"][s]])
                else:
                    self.tt("dve", to[0][:, 0:cn], self.banks[ob[0]][:, 0:cn], rz[0][:, 0:cn], ALU.mult,
                            [self.bres[ob[0]], r_rz[0]], [r_to])
                    self.tt("dve", to[1][:, 0:cn], self.banks[ob[1]][:, 0:cn], rz[1][:, 0:cn], ALU.mult,
                            [self.bres[ob[1]], r_rz[1]], [r_to], waw=False)
                    self.stt("dve", to[2][:, 0:cn], to[1][:, 0:cn], self.lam[:, 0:1], to[0][:, 0:cn], ALU.mult, ALU.add,
                             [r_to, self.r_vec], [r_to])
                    self.act(sqb[:, 0:cn], to[2][:, 0:cn], AF.Square, [r_to], [r_sqb])
                    b = SB[sbi[0] % 4]
                    sbi[0] += 1
                    self.mm(self.banks[b][:, 0:cn], self.onesb, sqb[:, 0:cn], True, True, [r_sqb, self.rc], [self.bres[b]])
                    self.rstd_from_sum(rso[:, 0:cn], self.banks[b][:, 0:cn], 128, [self.bres[b], self.rc], [r_rso])
                    self.stt("dve", ost["buf"][s][:, 0:cn], to[2][:, 0:cn], self.lam[:, 1:2], rso[:, 0:cn], ALU.mult, ALU.mult,
                             [r_to, r_rso, self.r_vec], [ost["res"][s]])
                P.dma("sp", ost["str"][s], oT[orow0:orow0 + 128, c0:c0 + cn], ost["buf"][s][:, 0:cn],
                      reads=[ost["res"][s]], writes=[roT], waw=False)

        for g in lim(range(4), 1):
            kv = cnt["kv"] % 2
            cnt["kv"] += 1
            self.load_kv(qk[(16 + g) * 128:(17 + g) * 128, :], rqk, pj[2560 + g * 128:2560 + (g + 1) * 128, :], rpj,
                         KTt[kv], VTt[kv], Vt[kv], r_k[kv], r_v[kv], s_k[kv], s_v[kv], [4, 5, 6, 7])
            for hq in lim(range(4 * g, 4 * g + 4), 1):
                attend(hq * 128, hq * 128, kv, 128 ** -0.5, [(0, 128)])
        for h in lim(range(8), 1):
            kv = cnt["kv"] % 2
            cnt["kv"] += 1
            self.load_kv(qk[(28 + h) * 128:(29 + h) * 128, :], rqk, pj[8192 + h * 128:8192 + (h + 1) * 128, :], rpj,
                         KTt[kv], VTt[kv], Vt[kv], r_k[kv], r_v[kv], s_k[kv], s_v[kv], [4, 5, 6, 7])
            attend((20 + h) * 128, 3072 + h * 128, kv, 64 ** -0.5, [(0, 64), (64, 128)])
        P.barrier()

    def phase_att_b(self, W):
        P = self.P
        P.phase = "phase_att_b"
        self.areset()
        pj, rpj = self.S["projT"], self.R["projT"]
        oT, roT = self.S["oT"], self.R["oT"]
        KTt = [self.bf16(T) for _ in range(2)]
        VTt = [self.bf16(T) for _ in range(2)]
        Vt = [self.bf16(T // 128, 128) for _ in range(2)]
        QTt = [self.bf16(T) for _ in range(2)]
        r_k = [Res() for _ in range(2)]
        r_v = [{"t": Res(), "v": Res()} for _ in range(2)]
        r_q = [Res() for _ in range(2)]
        s_k = [P.dstream(f"bk{i}") for i in range(2)]
        s_v = [P.dstream(f"bv{i}") for i in range(2)]
        s_q = [P.dstream(f"bq{i}") for i in range(2)]
        bt = self.f32(15, 64)
        EB = [self.f32(16, 64) for _ in range(2)]
        r_bt, r_eb = Res(), [Res() for _ in range(2)]
        s_bt = P.dstream("bbt")
        msk = self.f32(64)
        r_m = Res()
        P.dma("sp", s_bt, msk, self.I["maskB"], writes=[r_m])
        NE = 3
        ex = [self.f32(7 * 64) for _ in range(NE)]
        r_ex = [Res() for _ in range(NE)]
        Pt = [self.bf16(7 * 64) for _ in range(NE)]
        r_p = [Res() for _ in range(NE)]
        rz = self.f32(512)
        r_rz = Res()
        ost = self.store_stage("bo", 2, BF16)
        it = 0
        sbi = 0
        gi = 0
        btab = W["bias_tab"].rearrange("p (h d q) -> p h d q", h=8, d=15)
        for h in lim(range(8), 1):
            kv = h % 2
            self.load_kv(pj[4096 + h * 128:4096 + (h + 1) * 128, :], rpj, pj[5120 + h * 128:5120 + (h + 1) * 128, :], rpj,
                         KTt[kv], VTt[kv], Vt[kv], r_k[kv], r_v[kv], s_k[kv], s_v[kv], [0, 1, 2, 3])
            P.dma("sp", s_q[kv], QTt[kv], pj[3072 + h * 128:3072 + (h + 1) * 128, :], reads=[rpj], writes=[r_q[kv]])
            P.dma("sp", s_bt, bt, btab[:, h], writes=[r_bt])
            self.act(EB[kv][:, 0:15, :], bt, AF.Exp, [r_bt], [r_eb[kv]])
            self.tt("dve", EB[kv][:, 0:15, :], EB[kv][:, 0:15, :], msk.unsqueeze(1).broadcast_to([128, 15, 64]), ALU.mult,
                    [r_eb[kv], r_m], [r_eb[kv]])
            P.emit("dve", lambda e, t=EB[kv][:, 15, :]: e.memset(t, 0.0), (), [r_eb[kv]], waw=False)
            scale = 128 ** -0.5
            for g8 in lim(range(8), 1):
                obk, zbk = 4 + gi % 2, 6 + gi % 2
                gi += 1
                for ri in range(8):
                    r = g8 * 8 + ri
                    r0 = min(max(r - 4, 0), 56)
                    pbs = list(range(r0 // 2, (r0 + 7) // 2 + 1))
                    blocks = [("n", pb) for pb in pbs] + [("c", NL // 128), ("c", NL // 128 + 1)]
                    nblk = len(blocks)
                    e = it % NE
                    it += 1
                    sb = sbi % 4
                    sbi += 1
                    qsl = QTt[kv][:, r * 64:(r + 1) * 64]
                    for bi_, (kind, kb) in enumerate(blocks):
                        self.mm(self.banks[sb][:, bi_ * 64:(bi_ + 1) * 64], KTt[kv][:, kb * 128:(kb + 1) * 128], qsl, True, True,
                                [r_k[kv], r_q[kv]], [self.bres[sb]], waw=(bi_ == 0))
                    nn = len(pbs)
                    self.act(ex[e][:, 0:nn * 64], self.banks[sb][:, 0:nn * 64], AF.Exp, [self.bres[sb]], [r_ex[e]], scale=scale)
                    self.act(Pt[e][:, nn * 64:nblk * 64], self.banks[sb][:, nn * 64:nblk * 64], AF.Exp, [self.bres[sb]], [r_p[e]],
                             waw="p", scale=scale)
                    for bi_, pb in enumerate(pbs):
                        for half in range(2):
                            krow = 2 * pb + half
                            d = krow - r + 7
                            idx = d if (r0 <= krow <= r0 + 7) else 15
                            lo = half * 64
                            self.tt("dve" if half == 0 else "pool", Pt[e][lo:lo + 64, bi_ * 64:(bi_ + 1) * 64],
                                    ex[e][lo:lo + 64, bi_ * 64:(bi_ + 1) * 64], EB[kv][lo:lo + 64, idx, :], ALU.mult,
                                    [r_ex[e], r_eb[kv]], [r_p[e]], waw="p")
                    for bi_, (kind, kb) in enumerate(blocks):
                        first, last = bi_ == 0, bi_ == nblk - 1
                        self.mm(self.banks[obk][:, ri * 64:(ri + 1) * 64], Vt[kv][:, kb, :], Pt[e][:, bi_ * 64:(bi_ + 1) * 64],
                                first, last, [r_v[kv]["v"], r_p[e]], [self.bres[obk]], waw=(ri == 0 and first))
                    for bi_, (kind, kb) in enumerate(blocks):
                        first, last = bi_ == 0, bi_ == nblk - 1
                        self.mm(self.banks[zbk][:, ri * 64:(ri + 1) * 64], self.onesb, Pt[e][:, bi_ * 64:(bi_ + 1) * 64],
                                first, last, [r_p[e], self.rc], [self.bres[zbk]], waw=(ri == 0 and first))
                s = ost["i"] % ost["n"]
                ost["i"] += 1
                self.recip(rz, self.banks[zbk][:, :], [self.bres[zbk]], [r_rz])
                self.tt("dve", ost["buf"][s], self.banks[obk][:, :], rz, ALU.mult, [self.bres[obk], r_rz], [ost["res"][s]])
                P.dma("sp", ost["str"][s], oT[2048 + h * 128:2048 + (h + 1) * 128, g8 * 512:(g8 + 1) * 512], ost["buf"][s],
                      reads=[ost["res"][s]], writes=[roT], waw=False)
            obk, zbk = 4 + gi % 2, 6 + gi % 2
            gi += 1
            e = it % NE
            it += 1
            for bi_, kb in enumerate((NL // 128, NL // 128 + 1)):
                sb = sbi % 4
                sbi += 1
                self.mm(self.banks[sb][:, 0:NCX], KTt[kv][:, kb * 128:(kb + 1) * 128], QTt[kv][:, NL:T], True, True,
                        [r_k[kv], r_q[kv]], [self.bres[sb]])
                self.act(Pt[e][:, 0:NCX], self.banks[sb][:, 0:NCX], AF.Exp, [self.bres[sb]], [r_p[e]], scale=scale)
                self.mm(self.banks[obk][:, 0:NCX], Vt[kv][:, kb, :], Pt[e][:, 0:NCX], bi_ == 0, bi_ == 1,
                        [r_v[kv]["v"], r_p[e]], [self.bres[obk]], waw=(bi_ == 0))
                self.mm(self.banks[zbk][:, 0:NCX], self.onesb, Pt[e][:, 0:NCX], bi_ == 0, bi_ == 1,
                        [r_p[e], self.rc], [self.bres[zbk]], waw=(bi_ == 0))
            s = ost["i"] % ost["n"]
            ost["i"] += 1
            self.recip(rz[:, 0:NCX], self.banks[zbk][:, 0:NCX], [self.bres[zbk]], [r_rz])
            self.tt("dve", ost["buf"][s][:, 0:NCX], self.banks[obk][:, 0:NCX], rz[:, 0:NCX], ALU.mult,
                    [self.bres[obk], r_rz], [ost["res"][s]])
            P.dma("sp", ost["str"][s], oT[2048 + h * 128:2048 + (h + 1) * 128, NL:T], ost["buf"][s][:, 0:NCX],
                  reads=[ost["res"][s]], writes=[roT], waw=False)
        P.barrier()


def _rope_table(dim, rep):
    t = np.arange(NL)
    rows = (t // 64).astype(np.float32)
    cols = (t % 64).astype(np.float32)
    n = dim // 4
    freqs = (np.float32(10000.0) ** (-np.arange(n, dtype=np.float32) / np.float32(n))).astype(np.float32)
    ang = np.concatenate([rows[:, None] * freqs, cols[:, None] * freqs], axis=-1).astype(np.float32)
    cos, sin = np.cos(ang).astype(np.float32), np.sin(ang).astype(np.float32)
    tab = np.zeros((dim, 2, NL), np.float32)
    for p in range(dim):
        i = p // 2
        tab[p, 0] = cos[:, i]
        tab[p, 1] = -sin[:, i] if p % 2 == 0 else sin[:, i]
    return np.ascontiguousarray(np.tile(tab, (rep, 1, 1)))


def _consts():
    c = {}
    c["ropeA"] = _rope_table(128, 1)
    c["ropeC"] = _rope_table(64, 2)
    cc = np.arange(64)
    c0 = np.clip(cc - 8, 0, 48)
    m = ((cc[None, :] >= c0[:, None]) & (cc[None, :] < c0[:, None] + 16))
    c["maskB"] = np.ascontiguousarray(np.tile(m.T.astype(np.float32), (2, 1)))
    c["ident"] = np.eye(128, dtype=np.float32)
    p = np.zeros((128, 128), np.float32)
    for k in range(128):
        p[k, k ^ 1] = 1.0
    c["perm"] = p
    return c


def _pvec(v):
    return np.ascontiguousarray(np.asarray(v, np.float32).reshape(-1, 128).T)


def _layer_inputs(inp, layers):
    out = {}
    kc = np.arange(64)[:, None]
    qc = np.arange(64)[None, :]
    cidx = np.clip(kc - qc + 15, 0, 30)
    vec_l, lam_l, bt_l = [], [], []
    for l in layers:
        li = 0.8 - 0.6 * math.exp(-0.3 * l)
        vec = np.concatenate([
            _pvec(inp["ada_bias"][l]), _pvec(inp["norm_mix_pre"][l]), _pvec(inp["norm_mix_post"][l]),
            _pvec(inp["norm_ffn_pre"][l]), _pvec(inp["norm_ffn_post"][l]), _pvec(inp["b_gate"][l]),
            _pvec(inp["a_q_norm"][l]), _pvec(inp["a_k_norm"][l]), _pvec(inp["c_subln"][l]),
            np.full((128, 1), li, np.float32), np.full((128, 1), 1.0 - li, np.float32)], axis=1)
        assert vec.shape == (128, NVEC)
        vec_l.append(vec)
        lam_l.append(np.concatenate([inp["c_lambda_q1"][l], inp["c_lambda_k1"][l], inp["c_lambda_q2"][l], inp["c_lambda_k2"][l]]))
        rpb = np.asarray(inp["b_rel_bias"][l], np.float32)
        bt = rpb[:, :, cidx]
        bt = np.transpose(bt, (2, 0, 1, 3)).reshape(64, 8 * 15 * 64)
        bt_l.append(np.tile(bt, (2, 1)))
    out["vecs"] = np.ascontiguousarray(np.stack(vec_l)).astype(np.float32)
    out["lamv"] = np.ascontiguousarray(np.stack(lam_l)).astype(np.float32)
    out["bias_tab"] = np.ascontiguousarray(np.stack(bt_l)).astype(np.float32)
    for k in ("ada_down", "ada_up", "w_in", "w_gate_up", "w_branch", "w_out", "w_ffn_in", "w_ffn_out"):
        out[k] = np.ascontiguousarray(np.asarray(inp[k])[list(layers)], dtype=np.float32)
    return out


_CACHE = {}


def _program(n_layers, debug=()):
    key = (n_layers, tuple(debug))
    if key not in _CACHE:
        b = Builder(n_layers, debug)
        nc = b.build()
        _CACHE[key] = (nc, b)
    return _CACHE[key]


N_CORES = 4
LAYERS_PER_LAUNCH = 1


def kernel(**inp):
    x = np.asarray(inp["x"], np.float32)
    ctx = np.asarray(inp["ctx"], np.float32)
    c = np.asarray(inp["c"], np.float32)
    c_ctx = np.asarray(inp["c_ctx"], np.float32)
    consts = _consts()
    cur = [np.concatenate([x[b], ctx[b]], axis=0) for b in range(N_CORES)]
    cT = [np.ascontiguousarray(np.stack([_pvec(c[b]), _pvec(c_ctx)], axis=2).reshape(128, KT * 2)) for b in range(N_CORES)]
    nc, _ = _program(LAYERS_PER_LAUNCH)
    for l0 in range(0, DEPTH, LAYERS_PER_LAUNCH):
        lw = _layer_inputs(inp, range(l0, l0 + LAYERS_PER_LAUNCH))
        in_maps = []
        for b in range(N_CORES):
            m = {"xin": np.ascontiguousarray(cur[b]), "cT": cT[b]}
            m.update(consts)
            m.update(lw)
            in_maps.append(m)
        res = run_bass_kernel_spmd(nc, in_maps, core_ids=list(range(N_CORES)))
        cur = [np.asarray(res.results[b]["xout"]) for b in range(N_CORES)]
        del lw, in_maps
    return np.stack([cur[b][:NL] for b in range(N_CORES)], axis=0).astype(np.float32)
```
